# Optimizing a Trainium2 kernel written in Bass

```python
import jax, jax.numpy as jnp
from jax import lax
import numpy as np

D_MODEL = 1024
BATCH = 32
SEQ = 2048
DEPTH = 1

HEAD_DIM = 64
N_Q_HEADS = 8
N_KV_HEADS = 2
WINDOW = 128
ATT_BLOCK = 128
N_BUCKETS = 32
MAX_DISTANCE = 128
REC_HEADS = 4
REC_KEY_DIM = 128
REC_VAL_DIM = 128
REC_CHUNK = 64
D_FF = 2816
PLE_DIM = 256
EPS = 1e-6

ATT_Q_W = N_Q_HEADS * HEAD_DIM
ATT_KV_W = N_KV_HEADS * HEAD_DIM
REC_K_W = REC_HEADS * REC_KEY_DIM
REC_V_W = REC_HEADS * REC_VAL_DIM
IN_W = ATT_Q_W + 2 * ATT_KV_W + 2 * REC_K_W + 2 * REC_V_W + 2 * D_MODEL

kernel_name = "hybrid_swa_hgrn2_macaron_block"


def _split_points():
    widths = [ATT_Q_W, ATT_KV_W, ATT_KV_W, REC_K_W, REC_K_W, REC_V_W, REC_V_W, D_MODEL]
    return [int(v) for v in np.cumsum(widths)]


def rms_norm(x, g):
    xf = x.astype(jnp.float32)
    y = xf * lax.rsqrt(jnp.mean(xf * xf, axis=-1, keepdims=True) + EPS)
    return (y * g.astype(jnp.float32)).astype(x.dtype)


def swiglu(x, w_in, w_out):
    gate, up = jnp.split(x @ w_in, 2, axis=-1)
    return (jax.nn.silu(gate) * up) @ w_out


def t5_band_buckets():
    qi = np.arange(ATT_BLOCK)[:, None] + ATT_BLOCK
    kj = np.arange(2 * ATT_BLOCK)[None, :]
    dist = qi - kj
    n = np.maximum(dist, 0)
    max_exact = N_BUCKETS // 2
    large = max_exact + (np.log(np.maximum(n, 1) / max_exact)
                         / np.log(MAX_DISTANCE / max_exact)
                         * (N_BUCKETS - max_exact)).astype(np.int32)
    large = np.minimum(large, N_BUCKETS - 1)
    bucket = np.where(n < max_exact, n, large).astype(np.int32)
    return bucket, dist.astype(np.int32)


def sliding_window_attention(q, k, v, rel_table, sinks):
    B, S = q.shape[0], q.shape[1]
    nb = S // ATT_BLOCK
    G = N_Q_HEADS // N_KV_HEADS
    qb = q.reshape(B, nb, ATT_BLOCK, N_KV_HEADS, G, HEAD_DIM)

    def band(t):
        t = t.reshape(B, nb, ATT_BLOCK, N_KV_HEADS, HEAD_DIM)
        prev = jnp.pad(t, ((0, 0), (1, 0), (0, 0), (0, 0), (0, 0)))[:, :-1]
        return jnp.concatenate([prev, t], axis=2)

    kb, vb = band(k), band(v)
    s = jnp.einsum('bnqhgd,bnkhd->bnhgqk', qb, kb).astype(jnp.float32) * (HEAD_DIM ** -0.5)
    bucket, dist = t5_band_buckets()
    bias = rel_table.astype(jnp.float32)[bucket]
    bias = bias.transpose(2, 0, 1).reshape(N_KV_HEADS, G, ATT_BLOCK, 2 * ATT_BLOCK)
    key_pos = (jnp.arange(nb)[:, None, None] * ATT_BLOCK - ATT_BLOCK
               + jnp.arange(2 * ATT_BLOCK)[None, None, :])
    dist_j = jnp.asarray(dist)[None]
    valid = (dist_j >= 0) & (dist_j < WINDOW) & (key_pos >= 0)
    s = jnp.where(valid[None, :, None, None], s + bias, -jnp.inf)
    sink = sinks.astype(jnp.float32).reshape(N_KV_HEADS, G, 1, 1)
    m = jnp.maximum(jnp.max(s, axis=-1, keepdims=True), sink)
    e = jnp.exp(s - m)
    probs = e / (jnp.sum(e, axis=-1, keepdims=True) + jnp.exp(sink - m))
    o = jnp.einsum('bnhgqk,bnkhd->bnqhgd', probs.astype(v.dtype), vb)
    return o.reshape(B, S, ATT_Q_W)


def hgrn2_recurrence(q, f_logit, i, lb):
    B, S = q.shape[0], q.shape[1]
    nc = S // REC_CHUNK
    lbf = lb.astype(jnp.float32).reshape(REC_HEADS, REC_KEY_DIM)
    f = lbf + (1.0 - lbf) * jax.nn.sigmoid(f_logit.astype(jnp.float32))
    log_f = jnp.log(f)
    k = 1.0 - f

    def to_chunks(t):
        return t.reshape(B, nc, REC_CHUNK, REC_HEADS, t.shape[-1]).transpose(1, 0, 3, 2, 4)

    qc = to_chunks(q.astype(jnp.float32))
    kc = to_chunks(k)
    vc = to_chunks(i.astype(jnp.float32))
    gc = to_chunks(log_f)
    causal = jnp.tril(jnp.ones((REC_CHUNK, REC_CHUNK), dtype=bool))[:, :, None]

    def step(state, inp):
        qt, kt, vt, gt = inp
        b = jnp.cumsum(gt, axis=2)
        diff = b[:, :, :, None, :] - b[:, :, None, :, :]
        decay = jnp.exp(jnp.where(causal, diff, -jnp.inf))
        attn = jnp.einsum('bhtd,bhsd,bhtsd->bhts', qt, kt, decay)
        o = (jnp.einsum('bhts,bhsv->bhtv', attn, vt)
             + jnp.einsum('bhtd,bhdv->bhtv', qt * jnp.exp(b), state))
        b_last = b[:, :, -1:, :]
        new_state = (jnp.exp(b_last[:, :, 0, :])[..., None] * state
                     + jnp.einsum('bhsd,bhsv->bhdv', kt * jnp.exp(b_last - b), vt))
        return new_state, o

    s0 = jnp.zeros((B, REC_HEADS, REC_KEY_DIM, REC_VAL_DIM), jnp.float32)
    _, o = lax.scan(step, s0, (qc, kc, vc, gc))
    return o.transpose(1, 0, 3, 2, 4).reshape(B, S, REC_HEADS, REC_VAL_DIM).astype(q.dtype)


def setup_inputs(seed: int = 0) -> dict:
    key = jax.random.key(seed)
    ks = jax.random.split(key, 24)
    f32 = jnp.float32

    def nrm(k, shape, scale):
        return jax.random.normal(k, shape, f32) * scale

    def gain(k, shape):
        return 1.0 + 0.05 * jax.random.normal(k, shape, f32)

    return {
        "x": nrm(ks[0], (BATCH, SEQ, D_MODEL), 1.0),
        "p": nrm(ks[1], (DEPTH, BATCH, SEQ, PLE_DIM), 1.0),
        "rel_bias": nrm(ks[2], (N_BUCKETS, N_Q_HEADS), 0.5),
        "lb_param": nrm(ks[3], (DEPTH + 1, REC_K_W), 1.0),
        "norm_ffn1": gain(ks[4], (DEPTH, D_MODEL)),
        "w_ffn1_in": nrm(ks[5], (DEPTH, D_MODEL, 2 * D_FF), D_MODEL ** -0.5),
        "w_ffn1_out": nrm(ks[6], (DEPTH, D_FF, D_MODEL), D_FF ** -0.5),
        "norm_mix": gain(ks[7], (DEPTH, D_MODEL)),
        "w_in": nrm(ks[8], (DEPTH, D_MODEL, IN_W), D_MODEL ** -0.5),
        "attn_sinks": nrm(ks[9], (DEPTH, N_Q_HEADS), 1.0),
        "rec_norm": gain(ks[10], (DEPTH, REC_VAL_DIM)),
        "w_att_proj": nrm(ks[11], (DEPTH, ATT_Q_W, D_MODEL), ATT_Q_W ** -0.5),
        "w_rec_proj": nrm(ks[12], (DEPTH, REC_V_W, D_MODEL), REC_V_W ** -0.5),
        "w_out": nrm(ks[13], (DEPTH, D_MODEL, D_MODEL), D_MODEL ** -0.5),
        "norm_ffn2": gain(ks[14], (DEPTH, D_MODEL)),
        "w_ffn2_in": nrm(ks[15], (DEPTH, D_MODEL, 2 * D_FF), D_MODEL ** -0.5),
        "w_ffn2_out": nrm(ks[16], (DEPTH, D_FF, D_MODEL), D_FF ** -0.5),
        "norm_ple": gain(ks[17], (DEPTH, D_MODEL)),
        "w_ple_gate": nrm(ks[18], (DEPTH, D_MODEL, D_MODEL), D_MODEL ** -0.5),
        "w_ple_proj": nrm(ks[19], (DEPTH, PLE_DIM, D_MODEL), PLE_DIM ** -0.5),
        "norm_final": gain(ks[20], (D_MODEL,)),
    }


def reference(x, p, rel_bias, lb_param, norm_ffn1, w_ffn1_in, w_ffn1_out, norm_mix, w_in,
              attn_sinks, rec_norm, w_att_proj, w_rec_proj, w_out, norm_ffn2, w_ffn2_in,
              w_ffn2_out, norm_ple, w_ple_gate, w_ple_proj, norm_final):
    B, S = x.shape[0], x.shape[1]
    lower_bounds = jnp.cumsum(jax.nn.softmax(lb_param.astype(jnp.float32), axis=0), axis=0)
    splits = _split_points()
    h = x
    for layer in range(DEPTH):
        h = h + 0.5 * swiglu(rms_norm(h, norm_ffn1[layer]), w_ffn1_in[layer], w_ffn1_out[layer])

        u = rms_norm(h, norm_mix[layer])
        proj = u @ w_in[layer]
        aq, ak, av, rq, rf, ri, rg, ga, gb = jnp.split(proj, splits, axis=-1)

        att = sliding_window_attention(
            aq.reshape(B, S, N_Q_HEADS, HEAD_DIM),
            ak.reshape(B, S, N_KV_HEADS, HEAD_DIM),
            av.reshape(B, S, N_KV_HEADS, HEAD_DIM),
            rel_bias, attn_sinks[layer])

        rec = hgrn2_recurrence(
            rq.reshape(B, S, REC_HEADS, REC_KEY_DIM),
            rf.reshape(B, S, REC_HEADS, REC_KEY_DIM),
            ri.reshape(B, S, REC_HEADS, REC_VAL_DIM),
            lower_bounds[layer])
        rec = rms_norm(rec, rec_norm[layer]).reshape(B, S, REC_V_W) * jax.nn.sigmoid(rg)

        y_a = att @ w_att_proj[layer]
        y_b = rec @ w_rec_proj[layer]
        merged = jax.nn.sigmoid(ga) * y_a + jax.nn.sigmoid(gb) * y_b
        h = h + merged @ w_out[layer]

        h = h + 0.5 * swiglu(rms_norm(h, norm_ffn2[layer]), w_ffn2_in[layer], w_ffn2_out[layer])

        gate = jax.nn.sigmoid(rms_norm(h, norm_ple[layer]) @ w_ple_gate[layer])
        h = h + gate * (p[layer] @ w_ple_proj[layer])
    return rms_norm(h, norm_final)
```

```python
import numpy as np
from contextlib import ExitStack
import concourse.bass as bass
import concourse.mybir as mybir
from concourse.bass_utils import run_bass_kernel_spmd

F32 = mybir.dt.float32
BF16 = mybir.dt.bfloat16
AF = mybir.ActivationFunctionType
ALU = mybir.AluOpType

D = 1024
DFF = 2816
T = 512
EPS = 1e-6
PW = 4096
NRING = 3
NEG = -30000.0


class Buf:
    def __init__(self, name, ap):
        self.name = name
        self.ap = ap
        self.lw = None
        self.rd = {}
        self.dsem = {}
        self.dcount = {}

    def add_read(self, tok):
        s, v = tok
        k = id(s)
        if k not in self.rd or self.rd[k][1] < v:
            self.rd[k] = (s, v)


class Eng:
    def __init__(self, name, e, sem, is_pe=False):
        self.name = name
        self.e = e
        self.sem = sem
        self.is_pe = is_pe
        self.count = 0
        self.waited = {}

    def wait(self, tok):
        if tok is None:
            return
        s, v = tok
        if self.is_pe and s is self.sem:
            return
        k = id(s)
        if self.waited.get(k, 0) >= v:
            return
        self.e.wait_ge(s, v)
        self.waited[k] = v


class K:
    def __init__(self, nseq, S):
        self.nseq = nseq
        self.S = S
        self.nc = bass.Bass("TRN2", target_bir_lowering=False)
        self.es = ExitStack()
        self.nsem = 0
        self.bank_i = 0

    def sem(self, name):
        self.nsem += 1
        return self.es.enter_context(self.nc.semaphore(name))

    def sbt(self, name, shape, dt):
        return self.es.enter_context(self.nc.sbuf_tensor("sb_" + name, shape, dt))

    def mk(self, name, n, F, dt):
        t = self.sbt(name, [128, n, F], dt)
        return [Buf(f"{name}{i}", t[:, i, :]) for i in range(n)]

    def mk1(self, name, shape, dt):
        t = self.sbt(name, shape, dt)
        return Buf(name, t[:])

    def deps(self, E, reads, writes):
        for b in reads:
            E.wait(b.lw)
        for b in writes:
            E.wait(b.lw)
            for t in b.rd.values():
                E.wait(t)

    def done(self, tok, reads, writes):
        for b in reads:
            b.add_read(tok)
        for b in writes:
            b.lw = tok
            b.rd = {}

    def op(self, E, fn, reads=(), writes=(), inc=True):
        self.deps(E, reads, writes)
        ins = fn(E.e)
        if E.is_pe and not inc:
            tok = (E.sem, E.count + 1)
        else:
            E.count += 1
            ins.then_inc(E.sem, 1)
            tok = (E.sem, E.count)
        self.done(tok, reads, writes)
        return tok

    def dma(self, Q, out, in_, sb, reads=(), writes=(), **kw):
        self.deps(Q, reads, writes)
        if Q.name not in sb.dsem:
            sb.dsem[Q.name] = self.sem("d_" + sb.name + "_" + Q.name)
            sb.dcount[Q.name] = 0
        ins = Q.e.dma_start(out=out, in_=in_, **kw)
        sb.dcount[Q.name] += 16
        ins.then_inc(sb.dsem[Q.name], 16)
        tok = (sb.dsem[Q.name], sb.dcount[Q.name])
        self.done(tok, reads, writes)
        return tok

    def bank(self):
        b = self.banks[self.bank_i % 8]
        self.bank_i += 1
        return b


def build(nseq, S, stop_after=None):
    kb = K(nseq, S)
    nc = kb.nc
    NT = S // T
    ntok = nseq * S

    def din(name, shape):
        return nc.dram_tensor(name, shape, F32, kind="ExternalInput").ap()

    x = din("x", [ntok, D])
    p = din("p", [ntok, 256])
    w1i = din("w1i", [D, 2 * DFF]); w1o = din("w1o", [DFF, D])
    w2i = din("w2i", [D, 2 * DFF]); w2o = din("w2o", [DFF, D])
    win = din("win", [D, 4864])
    watt = din("watt", [512, D]); wrec = din("wrec", [512, D]); wout = din("wout", [D, D])
    wpg = din("wpg", [D, D]); wpp = din("wpp", [256, D])
    gains = din("gains", [128, 40])
    recg_d = din("recg", [128, 1])
    lbp = din("lbp", [128, 8])
    sinks_d = din("sinks", [128, 4])
    biasm_d = din("biasm", [128, 2048])
    ident_d = din("ident", [128, 128])
    cmask_d = din("cmask", [128, 128])
    y = nc.dram_tensor("y", [ntok, D], F32, kind="ExternalOutput").ap()

    with kb.es:
        PE = Eng("pe", nc.tensor, kb.sem("s_pe"), is_pe=True)
        ACT = Eng("act", nc.scalar, kb.sem("s_act"))
        DVE = Eng("dve", nc.vector, kb.sem("s_dve"))
        SPQ = Eng("sp", nc.sync, None)
        GPQ = Eng("gp", nc.gpsimd, kb.sem("s_gp"))
        op, dma, mk, mk1 = kb.op, kb.dma, kb.mk, kb.mk1

        def kv(w):
            return w.rearrange("(k p) c -> p k c", p=128)

        pieces = []

        def ffn_in_piece(w, j):
            halves = []
            for e in range(2):
                srcs = []
                for gu in range(2):
                    col = gu * DFF + (2 * j + e) * 128
                    srcs.append((lambda v, gu=gu: v.rearrange("p (k q c) -> p k q c", k=8, q=2)[:, :, gu, :],
                                 kv(w)[:, :, col:col + 128], 0, 128))
                halves.append((e * 2048, 2048, srcs))
            return (4096, halves)

        def ffn_out_piece(w, m):
            halves = []
            for e in range(2):
                halves.append((e * 1408, 1408, [(lambda v: v.rearrange("p (k c) -> p k c", k=11),
                                                 kv(w)[:, e * 11:(e + 1) * 11, m * 128:(m + 1) * 128], 0, 128)]))
            return (2816, halves)

        def k8_piece(w, c0, width):
            hw = 4 * width
            if hw * 2 <= 2048:
                return (8 * width, [(0, 8 * width, [(lambda v: v.rearrange("p (k c) -> p k c", k=8),
                                                       kv(w)[:, :, c0:c0 + width], 0, 128)])])
            halves = []
            for e in range(2):
                halves.append((e * hw, hw, [(lambda v: v.rearrange("p (k c) -> p k c", k=4),
                                              kv(w)[:, e * 4:(e + 1) * 4, c0:c0 + width], 0, 128)]))
            return (8 * width, halves)

        def q_piece():
            halves = []
            for i in range(2):
                srcs = []
                for cc in range(2):
                    c = 2 * i + cc
                    for hf in range(2):
                        col = (hf * 4 + c) * 64
                        srcs.append((lambda v, cc=cc, hf=hf: v.rearrange("p (c k h d) -> p c k h d", c=2, k=8, h=2)[:, cc, :, hf, :],
                                     kv(win)[:, :, col:col + 64], 0, 128))
                halves.append((i * 2048, 2048, srcs))
            return (4096, halves)

        def ph2_piece(m):
            h0 = []
            for gab in range(2):
                col = 2816 + gab * 1024 + m * 128
                h0.append((lambda v, gab=gab: v.rearrange("p (k q c) -> p k q c", k=8, q=2)[:, :, gab, :],
                           kv(win)[:, :, col:col + 128], 0, 128))
            h1 = []
            wa = watt.rearrange("(hf c d) m -> hf d c m", hf=2, c=4)
            for hf in range(2):
                h1.append((lambda v: v[:, 0:512].rearrange("p (c m) -> p c m", c=4),
                           wa[hf][:, :, m * 128:(m + 1) * 128], hf * 64, hf * 64 + 64))
            h1.append((lambda v: v[:, 512:1024].rearrange("p (c m) -> p c m", c=4),
                       kv(wrec)[:, :, m * 128:(m + 1) * 128], 0, 128))
            return (3072, [(0, 2048, h0), (2048, 1024, h1)])

        names = []

        def add(name, pc):
            names.append(name)
            pieces.append(pc)

        for j in range(11): add(f"f1i{j}", ffn_in_piece(w1i, j))
        for m in range(8): add(f"f1o{m}", ffn_out_piece(w1o, m))
        add("rf", k8_piece(win, 1280, 512))
        add("q", q_piece())
        add("kv", k8_piece(win, 512, 256))
        add("ri", k8_piece(win, 1792, 512))
        add("rq", k8_piece(win, 768, 512))
        add("rg", k8_piece(win, 2304, 512))
        for m in range(8): add(f"ph2{m}", ph2_piece(m))
        for i in range(2): add(f"wo{i}", k8_piece(wout, i * 512, 512))
        for j in range(11): add(f"f2i{j}", ffn_in_piece(w2i, j))
        for m in range(8): add(f"f2o{m}", ffn_out_piece(w2o, m))
        for i in range(2): add(f"pg{i}", k8_piece(wpg, i * 512, 512))
        add("pp", (2048, [(0, 2048, [(lambda v: v.rearrange("p (k c) -> p k c", k=2), kv(wpp), 0, 128)])]))
        NP = len(pieces)
        pidx = {n: i for i, n in enumerate(names)}
        scr = nc.dram_tensor("scr", [NP, 128, PW], BF16).ap()
        scrB = [Buf(f"scr{i}", scr[i]) for i in range(NP)]

        ring = mk("ring", NRING, PW, BF16)
        xb = mk("xb", 2, D, F32)
        ob = mk("ob", 2, D, F32)
        pin = mk1("pin", [128, 4, 256], F32)
        hT = mk("hT", 8, T, F32)
        uT = mk("uT", 8, T, BF16)
        sq = mk("sq", 4, T, BF16)
        pT = mk("pT", 2, T, BF16)
        aT_t = kb.sbt("aT", [128, 22, T], BF16)
        aT = [Buf(f"aT{i}", aT_t[:, i, :]) for i in range(22)]
        sg = mk("sg", 3, T, F32)
        ident = mk1("ident", [128, 128], F32)
        identb = mk1("identb", [128, 128], BF16)
        onesb = mk1("onesb", [128, 128], BF16)
        cmask = mk1("cmask", [128, 128], F32)
        rmask = mk1("rmask", [128, T], F32)
        gn = mk1("gn", [128, 40], F32)
        recg = mk1("recg", [128, 1], F32)
        lbt = mk1("lbt", [128, 8], F32)
        oml = mk1("oml", [128, 4], F32)
        noml = mk1("noml", [128, 4], F32)
        sinke = mk1("sinke", [128, 4], F32)
        biasm = mk1("biasm", [128, 2048], BF16)
        qTm = [mk1(f"qTm{i}", [128, 4, 4, 128], BF16) for i in range(2)]
        kTc = mk1("kTc", [128, 128], BF16)
        kTt = mk1("kTt", [128, T], BF16)
        vc = mk1("vc", [128, 128], BF16)
        vt = mk1("vt", [128, 4, 128], BF16)
        PT = mk("PT", 4, T, BF16)
        attT = mk1("attT", [128, 4, T], BF16)
        dens = mk1("dens", [128, T], F32)
        sn = mk("sn", 4, T, F32)
        bT = mk1("bT", [128, T], F32)
        Eb = mk1("Eb", [128, T], F32)
        Enb = mk1("Enb", [128, T], F32)
        tmp = mk1("tmp", [128, T], F32)
        khT = mk("khT", 2, T, BF16)
        ktT = mk("ktT", 2, T, BF16)
        qh = mk("qh", 2, T, BF16)
        khtok = mk("khtok", 2, T, BF16)
        am = mk("am", 2, T, BF16)
        Ea = mk("Ea", 4, 8, F32)
        Sst = mk("Sst", 4, 128, F32)
        Sb_t = kb.sbt("Sb", [128, 4, 9, 128], BF16)
        Sb = [[Buf(f"Sb{h}_{c}", Sb_t[:, h, c, :]) for c in range(9)] for h in range(4)]
        vrh = [mk(f"vrh{i}", 4, T, BF16) for i in range(2)]
        lf = mk1("lf", [128, T], F32)
        oT = mk1("oT", [128, T], F32)
        osq = mk1("osq", [128, T], BF16)
        t1 = mk1("t1", [128, T], F32)
        sgr = mk("sgr", 2, T, F32)
        recT = mk("recT", 4, T, BF16)
        sga = mk1("sga", [128, T], F32)
        sgb = mk1("sgb", [128, T], F32)
        m1 = mk1("m1", [128, T], F32)
        m2 = mk1("m2", [128, T], F32)
        mT = mk("mT", 8, T, BF16)
        kb.banks = [Buf(f"bank{i}", kb.es.enter_context(nc.psum_tensor(f"bank{i}", [128, T], F32))[:]) for i in range(8)]

        dma(GPQ, ident.ap, ident_d, ident, writes=[ident])
        dma(GPQ, cmask.ap, cmask_d, cmask, writes=[cmask])
        dma(GPQ, gn.ap, gains, gn, writes=[gn])
        dma(GPQ, recg.ap, recg_d, recg, writes=[recg])
        dma(GPQ, lbt.ap, lbp, lbt, writes=[lbt])
        dma(GPQ, sinke.ap, sinks_d, sinke, writes=[sinke])
        stg_f = aT_t[:].rearrange("p a b -> p (a b)").bitcast(F32)
        stg = [Buf("stg0", stg_f[:, 0:2048]), Buf("stg1", stg_f[:, 2048:4096])]
        dma(GPQ, stg[0].ap, biasm_d, stg[0], writes=[stg[0]])
        op(DVE, lambda e: e.tensor_copy(out=biasm.ap, in_=stg[0].ap), reads=[stg[0]], writes=[biasm])
        op(DVE, lambda e: e.tensor_copy(out=identb.ap, in_=ident.ap), reads=[ident], writes=[identb])
        op(DVE, lambda e: e.memset(onesb.ap, 1.0), writes=[onesb])
        op(DVE, lambda e: e.memset(rmask.ap, 1.0), writes=[rmask])
        op(DVE, lambda e: e.memset(rmask.ap.rearrange("p (c t) -> p c t", t=64)[:, :, 0:1], 0.0), writes=[rmask])
        for i in range(2):
            op(DVE, lambda e, i=i: e.memset(qTm[i].ap, 0.0), writes=[qTm[i]])
            for n in range(4):
                op(DVE, lambda e, i=i, n=n: e.memset(vrh[i][n].ap, 0.0), writes=[vrh[i][n]])
        op(DVE, lambda e: e.tensor_tensor(out=oml.ap, in0=lbt.ap[:, 4:8], in1=lbt.ap[:, 0:4], op=ALU.subtract), reads=[lbt], writes=[oml])
        op(ACT, lambda e: e.activation(out=oml.ap, in_=oml.ap, func=AF.Sigmoid), reads=[oml], writes=[oml])
        op(DVE, lambda e: e.tensor_scalar(out=noml.ap, in0=oml.ap, scalar1=-1.0, scalar2=None, op0=ALU.mult), reads=[oml], writes=[noml])
        op(ACT, lambda e: e.activation(out=sinke.ap, in_=sinke.ap, func=AF.Exp), reads=[sinke], writes=[sinke])

        hcount = 0
        for i, (n, halves) in enumerate(pieces):
            slot = ring[i % NRING]
            for (off, size, srcs) in halves:
                st = stg[hcount % 2]
                for (dfn, src, plo, phi) in srcs:
                    dst = dfn(st.ap[plo:phi, 0:size])
                    dma(SPQ, dst, src, st, writes=[st])
                CE = DVE if hcount % 2 == 0 else ACT
                if CE is DVE:
                    op(DVE, lambda e, st=st, off=off, size=size: e.tensor_copy(out=slot.ap[:, off:off + size], in_=st.ap[:, 0:size]), reads=[st], writes=[slot])
                else:
                    op(ACT, lambda e, st=st, off=off, size=size: e.activation(out=slot.ap[:, off:off + size], in_=st.ap[:, 0:size], func=AF.Copy), reads=[st], writes=[slot])
                hcount += 1
            dma(GPQ, scr[i][:, 0:n], slot.ap[:, 0:n], slot, reads=[slot], writes=[scrB[i]])
        for b in aT:
            for s in stg:
                if s.lw: b.add_read(s.lw)
                for t_ in s.rd.values(): b.add_read(t_)

        wstate = {"i": 0}

        def load_piece(name):
            i = pidx[name]
            n = pieces[i][0]
            slot = ring[wstate["i"] % NRING]
            wstate["i"] += 1
            dma(SPQ, slot.ap[:, 0:n], scr[i][:, 0:n], slot, reads=[scrB[i]], writes=[slot])
            return slot

        def mm(bankb, out, lhsT, rhs, start, stop, reads, inc=False):
            op(PE, lambda e: e.matmul(out, lhsT=lhsT, rhs=rhs, start=start, stop=stop), reads=reads, writes=[bankb], inc=inc)

        def norm(gi, out_bufs, out_is_h=False):
            bk = kb.bank()
            for c in range(8):
                s = sq[c % 4]
                op(ACT, lambda e, c=c, s=s: e.activation(out=s.ap, in_=hT[c].ap, func=AF.Square), reads=[hT[c]], writes=[s])
                mm(bk, bk.ap, onesb.ap, s.ap, c == 0, c == 7, [onesb, s], inc=True)
            op(ACT, lambda e: e.activation(out=bk.ap, in_=bk.ap, func=AF.Ln, scale=1.0 / D, bias=EPS), reads=[bk], writes=[bk])
            op(ACT, lambda e: e.activation(out=bk.ap, in_=bk.ap, func=AF.Exp, scale=-0.5), reads=[bk], writes=[bk])
            for c in range(8):
                o = out_bufs[c]
                op(DVE, lambda e, c=c, o=o: e.scalar_tensor_tensor(out=o.ap, in0=hT[c].ap, scalar=gn.ap[:, gi * 8 + c:gi * 8 + c + 1],
                                                                   in1=bk.ap, op0=ALU.mult, op1=ALU.mult),
                   reads=[hT[c], gn, bk], writes=[o])

        def ffn(pre, gi):
            norm(gi, uT)
            for j in range(11):
                slot = load_piece(f"{pre}i{j}")
                for e_ in range(2):
                    ch = 2 * j + e_
                    v4 = slot.ap[:, e_ * 2048:(e_ + 1) * 2048].rearrange("p (k q c) -> p k q c", k=8, q=2)
                    bg = kb.bank()
                    for k in range(8):
                        mm(bg, bg.ap, v4[:, k, 0, :], uT[k].ap, k == 0, k == 7, [slot, uT[k]], inc=(k == 7))
                    s = sg[ch % 3]
                    op(ACT, lambda e, bg=bg, s=s: e.activation(out=s.ap, in_=bg.ap, func=AF.Silu), reads=[bg], writes=[s])
                    bu = kb.bank()
                    for k in range(8):
                        mm(bu, bu.ap, v4[:, k, 1, :], uT[k].ap, k == 0, k == 7, [slot, uT[k]], inc=(k == 7))
                    op(DVE, lambda e, bu=bu, s=s, ch=ch: e.tensor_tensor(out=aT[ch].ap, in0=bu.ap, in1=s.ap, op=ALU.mult), reads=[bu, s], writes=[aT[ch]])
            for m in range(8):
                slot = load_piece(f"{pre}o{m}")
                v3 = slot.ap[:, 0:2816].rearrange("p (k c) -> p k c", k=22)
                bk = kb.bank()
                for k in range(22):
                    mm(bk, bk.ap, v3[:, k, :], aT[k].ap, k == 0, k == 21, [slot, aT[k]], inc=(k == 21))
                op(DVE, lambda e, bk=bk, m=m: e.scalar_tensor_tensor(out=hT[m].ap, in0=bk.ap, scalar=0.5, in1=hT[m].ap, op0=ALU.mult, op1=ALU.add),
                   reads=[bk, hT[m]], writes=[hT[m]])

        def load_x_block(tok0, n):
            b = xb[n % 2]
            dma(GPQ, b.ap, x[tok0 + n * 128: tok0 + (n + 1) * 128, :], b, writes=[b])

        tiles = [(s_, j) for s_ in range(nseq) for j in range(NT)]
        load_x_block(0, 0)
        load_x_block(0, 1)
        dma(GPQ, pin.ap, p[0:T, :].rearrange("(n q) d -> q n d", q=128), pin, writes=[pin])
        out_toks = []
        for ti, (s_, j) in enumerate(tiles):
            tok0 = s_ * S + j * T
            first = (j == 0)
            xbanks = [kb.bank() for _ in range(8)]
            for n in range(4):
                b = xb[n % 2]
                for c in range(8):
                    op(PE, lambda e, c=c, n=n, b=b: e.transpose(xbanks[c].ap[:, n * 128:(n + 1) * 128], b.ap[:, c * 128:(c + 1) * 128], ident.ap),
                       reads=[b, ident], writes=[xbanks[c]], inc=(c == 7))
                if n < 2:
                    load_x_block(tok0, n + 2)
                elif ti + 1 < len(tiles):
                    s2, j2 = tiles[ti + 1]
                    load_x_block(s2 * S + j2 * T, n - 2)
            for c in range(8):
                if c % 2 == 0:
                    op(DVE, lambda e, c=c: e.tensor_copy(out=hT[c].ap, in_=xbanks[c].ap), reads=[xbanks[c]], writes=[hT[c]])
                else:
                    op(ACT, lambda e, c=c: e.activation(out=hT[c].ap, in_=xbanks[c].ap, func=AF.Copy), reads=[xbanks[c]], writes=[hT[c]])
            for c in range(2):
                bk = kb.bank()
                for n in range(4):
                    op(PE, lambda e, c=c, n=n, bk=bk: e.transpose(bk.ap[:, n * 128:(n + 1) * 128], pin.ap[:, n, c * 128:(c + 1) * 128], ident.ap),
                       reads=[pin, ident], writes=[bk], inc=(n == 3))
                op(DVE, lambda e, c=c, bk=bk: e.tensor_copy(out=pT[c].ap, in_=bk.ap), reads=[bk], writes=[pT[c]])
            if ti + 1 < len(tiles):
                s2, j2 = tiles[ti + 1]
                t2 = s2 * S + j2 * T
                dma(GPQ, pin.ap, p[t2:t2 + T, :].rearrange("(n q) d -> q n d", q=128), pin, writes=[pin])

            if stop_after is None or stop_after >= 1:
                ffn("f1", 0)

            if stop_after is None or stop_after >= 2:
                norm(1, uT)
                if first:
                    for h in range(4):
                        op(DVE, lambda e, h=h: e.memset(Sst[h].ap, 0.0), writes=[Sst[h]])
                        op(DVE, lambda e, h=h: e.memset(Sb[h][0].ap, 0.0), writes=[Sb[h][0]])
                slot = load_piece("rf")
                v3 = slot.ap.rearrange("p (k c) -> p k c", k=8)
                for h in range(4):
                    bk = kb.bank()
                    for k in range(8):
                        mm(bk, bk.ap, v3[:, k, h * 128:(h + 1) * 128], uT[k].ap, k == 0, k == 7, [slot, uT[k]], inc=(k == 7))
                    op(ACT, lambda e, bk=bk, h=h: e.activation(out=sn[h].ap, in_=bk.ap, func=AF.Sigmoid, scale=-1.0), reads=[bk], writes=[sn[h]])
                slot = load_piece("q")
                v4 = slot.ap.rearrange("p (c k m) -> p c k m", c=4, k=8)
                for c in range(4):
                    bk = kb.bank()
                    for k in range(8):
                        mm(bk, bk.ap, v4[:, c, k, :], uT[k].ap, k == 0, k == 7, [slot, uT[k]], inc=(k == 7))
                    for i in range(2):
                        op(ACT, lambda e, bk=bk, c=c, i=i: e.activation(out=qTm[i].ap[i * 64:(i + 1) * 64, :, c, :],
                                                                        in_=bk.ap[i * 64:(i + 1) * 64, :].rearrange("p (n q) -> p n q", n=4),
                                                                        func=AF.Copy, scale=0.125), reads=[bk], writes=[qTm[i]])
                slot = load_piece("kv")
                v3 = slot.ap[:, 0:2048].rearrange("p (k c) -> p k c", k=8)
                bk = kb.bank()
                for k in range(8):
                    mm(bk, bk.ap, v3[:, k, 0:128], uT[k].ap, k == 0, k == 7, [slot, uT[k]], inc=(k == 7))
                op(DVE, lambda e, bk=bk: e.tensor_copy(out=kTt.ap, in_=bk.ap), reads=[bk], writes=[kTt])
                bk = kb.bank()
                for n in range(4):
                    for k in range(8):
                        mm(bk, bk.ap[:, n * 128:(n + 1) * 128], uT[k].ap[:, n * 128:(n + 1) * 128], v3[:, k, 128:256], k == 0, k == 7, [slot, uT[k]],
                           inc=(k == 7 and n == 3))
                op(DVE, lambda e, bk=bk: e.tensor_copy(out=vt.ap.rearrange("p n d -> p (n d)"), in_=bk.ap), reads=[bk], writes=[vt])
                slot = load_piece("ri")
                v3 = slot.ap.rearrange("p (k c) -> p k c", k=8)
                for n in range(4):
                    bk = kb.bank()
                    for k in range(8):
                        mm(bk, bk.ap, uT[k].ap[:, n * 128:(n + 1) * 128], v3[:, k, :], k == 0, k == 7, [slot, uT[k]], inc=(k == 7))
                    for i in range(2):
                        op(ACT, lambda e, bk=bk, n=n, i=i: e.activation(out=vrh[i][n].ap[i * 64:(i + 1) * 64, :], in_=bk.ap[i * 64:(i + 1) * 64, :], func=AF.Copy),
                           reads=[bk], writes=[vrh[i][n]])

                bm5 = biasm.ap.rearrange("p (a b g q) -> p a b g q", a=2, b=2, g=4)
                pti = 0
                for n in range(4):
                    gb_ = j * 4 + n
                    parts = ([0] if gb_ > 0 else []) + [1]
                    numb = kb.bank()
                    denb = kb.bank()
                    for kvh in range(2):
                        lo, hi = kvh * 64, kvh * 64 + 64
                        pts = []
                        for part in parts:
                            if part == 0:
                                if n == 0:
                                    kb_, kap = kTc, kTc.ap
                                    vb_, vap = vc, vc.ap[:, lo:hi]
                                else:
                                    kb_, kap = kTt, kTt.ap[:, (n - 1) * 128:n * 128]
                                    vb_, vap = vt, vt.ap[:, n - 1, lo:hi]
                            else:
                                kb_, kap = kTt, kTt.ap[:, n * 128:(n + 1) * 128]
                                vb_, vap = vt, vt.ap[:, n, lo:hi]
                            bk = kb.bank()
                            mm(bk, bk.ap, identb.ap, bm5[:, kvh, part].rearrange("p g q -> p (g q)"), True, False, [identb, biasm])
                            mm(bk, bk.ap, kap, qTm[kvh].ap[:, n].rearrange("p g q -> p (g q)"), False, True, [kb_, qTm[kvh]], inc=True)
                            P_ = PT[pti % 4]
                            pti += 1
                            op(ACT, lambda e, bk=bk, P_=P_: e.activation(out=P_.ap, in_=bk.ap, func=AF.Exp), reads=[bk], writes=[P_])
                            pts.append((P_, vb_, vap))
                        for pi, (P_, vb_, vap) in enumerate(pts):
                            mm(numb, numb.ap[lo:hi, :], vap, P_.ap, pi == 0, pi == len(pts) - 1, [vb_, P_])
                        for pi, (P_, vb_, vap) in enumerate(pts):
                            mm(denb, denb.ap[lo:hi, :], onesb.ap[:, 0:64], P_.ap, pi == 0, pi == len(pts) - 1, [onesb, P_], inc=(pi == len(pts) - 1))
                    op(DVE, lambda e, denb=denb: e.tensor_tensor(out=dens.ap.rearrange("p (g q) -> p g q", g=4), in0=denb.ap.rearrange("p (g q) -> p g q", g=4),
                                                               in1=sinke.ap.unsqueeze(2).broadcast_to([128, 4, 128]), op=ALU.add),
                       reads=[denb, sinke], writes=[dens])
                    op(ACT, lambda e: e.activation(out=dens.ap, in_=dens.ap, func=AF.Ln), reads=[dens], writes=[dens])
                    op(ACT, lambda e: e.activation(out=dens.ap, in_=dens.ap, func=AF.Exp, scale=-1.0), reads=[dens], writes=[dens])
                    op(DVE, lambda e, numb=numb, n=n: e.tensor_tensor(out=attT.ap[:, :, n * 128:(n + 1) * 128], in0=numb.ap.rearrange("p (g q) -> p g q", g=4),
                                                                      in1=dens.ap.rearrange("p (g q) -> p g q", g=4), op=ALU.mult),
                       reads=[numb, dens], writes=[attT])
                op(DVE, lambda e: e.tensor_copy(out=kTc.ap, in_=kTt.ap[:, 384:512]), reads=[kTt], writes=[kTc])
                op(DVE, lambda e: e.tensor_copy(out=vc.ap, in_=vt.ap[:, 3, :]), reads=[vt], writes=[vc])

                slot_rq = load_piece("rq")
                vq = slot_rq.ap.rearrange("p (k c) -> p k c", k=8)
                slot_rg = load_piece("rg")
                vg = slot_rg.ap.rearrange("p (k c) -> p k c", k=8)
                for h in range(4):
                    op(ACT, lambda e, h=h: e.activation(out=lf.ap, in_=sn[h].ap, func=AF.Ln, scale=noml.ap[:, h:h + 1], bias=1.0), reads=[sn[h], noml], writes=[lf])
                    op(DVE, lambda e: e.tensor_tensor_scan(out=bT.ap, data0=rmask.ap, data1=lf.ap, initial=0.0, op0=ALU.mult, op1=ALU.add), reads=[rmask, lf], writes=[bT])
                    op(ACT, lambda e: e.activation(out=Eb.ap, in_=bT.ap, func=AF.Exp), reads=[bT], writes=[Eb])
                    op(ACT, lambda e: e.activation(out=Enb.ap, in_=bT.ap, func=AF.Exp, scale=-1.0), reads=[bT], writes=[Enb])
                    bq = kb.bank()
                    for k in range(8):
                        mm(bq, bq.ap, vq[:, k, h * 128:(h + 1) * 128], uT[k].ap, k == 0, k == 7, [slot_rq, uT[k]], inc=(k == 7))
                    op(DVE, lambda e, bq=bq, h=h: e.tensor_tensor(out=qh[h % 2].ap, in0=bq.ap, in1=Eb.ap, op=ALU.mult), reads=[bq, Eb], writes=[qh[h % 2]])
                    eb3 = Eb.ap.rearrange("p (c t) -> p c t", t=64)
                    op(DVE, lambda e, h=h, eb3=eb3: e.tensor_copy(out=Ea[h].ap, in_=eb3[:, :, 63]), reads=[Eb], writes=[Ea[h]])
                    op(DVE, lambda e, eb3=eb3: e.tensor_tensor(out=tmp.ap.rearrange("p (c t) -> p c t", t=64), in0=Enb.ap.rearrange("p (c t) -> p c t", t=64),
                                                               in1=eb3[:, :, 63:64].broadcast_to([128, 8, 64]), op=ALU.mult), reads=[Enb, Eb], writes=[tmp])
                    kh_, kt_ = khT[h % 2], ktT[h % 2]
                    op(DVE, lambda e, h=h, kh_=kh_: e.scalar_tensor_tensor(out=kh_.ap, in0=sn[h].ap, scalar=oml.ap[:, h:h + 1], in1=tmp.ap, op0=ALU.mult, op1=ALU.mult),
                       reads=[sn[h], oml, tmp], writes=[kh_])
                    op(DVE, lambda e, h=h, kt_=kt_: e.scalar_tensor_tensor(out=kt_.ap, in0=sn[h].ap, scalar=oml.ap[:, h:h + 1], in1=Enb.ap, op0=ALU.mult, op1=ALU.mult),
                       reads=[sn[h], oml, Enb], writes=[kt_])
                    bt_ = kb.bank()
                    btb = bt_.ap.bitcast(BF16)
                    for n in range(4):
                        op(PE, lambda e, n=n, kh_=kh_, btb=btb: e.transpose(btb[:, n * 128:(n + 1) * 128], kh_.ap[:, n * 128:(n + 1) * 128], identb.ap),
                           reads=[kh_, identb], writes=[bt_], inc=(n == 3))
                    op(ACT, lambda e, btb=btb, h=h: e.activation(out=khtok[h % 2].ap, in_=btb[:, 0:T], func=AF.Copy), reads=[bt_], writes=[khtok[h % 2]])
                    ba = kb.bank()
                    for n in range(4):
                        mm(ba, ba.ap[:, n * 128:(n + 1) * 128], kt_.ap[:, n * 128:(n + 1) * 128], qh[h % 2].ap[:, n * 128:(n + 1) * 128], True, True, [kt_, qh[h % 2]], inc=(n == 3))
                    op(DVE, lambda e, ba=ba, h=h: e.tensor_tensor(out=am[h % 2].ap.rearrange("p (n t) -> p n t", n=4), in0=ba.ap.rearrange("p (n t) -> p n t", n=4),
                                                                  in1=cmask.ap.unsqueeze(1).broadcast_to([128, 4, 128]), op=ALU.mult),
                       reads=[ba, cmask], writes=[am[h % 2]])
                    kt3 = khtok[h % 2].ap.rearrange("p (n d) -> p n d", n=4)
                    dbanks = [kb.bank(), kb.bank()]
                    for c in range(8):
                        n, half = c // 2, c % 2
                        db = dbanks[c // 4]
                        mm(db, db.ap[:, (c % 4) * 128:(c % 4 + 1) * 128], kt3[:, n, :],
                           vrh[half][n].ap[:, h * 128:(h + 1) * 128], True, True, [khtok[h % 2], vrh[half][n]], inc=(c % 4 == 3))
                    for c in range(8):
                        db = dbanks[c // 4]
                        op(DVE, lambda e, c=c, db=db, h=h: e.scalar_tensor_tensor(out=Sst[h].ap, in0=Sst[h].ap, scalar=Ea[h].ap[:, c:c + 1],
                                                                                  in1=db.ap[:, (c % 4) * 128:(c % 4 + 1) * 128], op0=ALU.mult, op1=ALU.add),
                           reads=[Sst[h], Ea[h], db], writes=[Sst[h]])
                        op(ACT, lambda e, c=c, h=h: e.activation(out=Sb[h][c + 1].ap, in_=Sst[h].ap, func=AF.Copy), reads=[Sst[h]], writes=[Sb[h][c + 1]])
                    bo = kb.bank()
                    for c in range(8):
                        n, half = c // 2, c % 2
                        mm(bo, bo.ap[:, c * 64:(c + 1) * 64], Sb[h][c].ap, qh[h % 2].ap[:, c * 64:(c + 1) * 64], True, False, [Sb[h][c], qh[h % 2]])
                        mm(bo, bo.ap[:, c * 64:(c + 1) * 64], vrh[half][n].ap[:, h * 128:(h + 1) * 128], am[h % 2].ap[:, n * 128 + half * 64:n * 128 + half * 64 + 64],
                           False, True, [vrh[half][n], am[h % 2]], inc=(c == 7))
                    op(ACT, lambda e, h=h: e.activation(out=Sb[h][0].ap, in_=Sst[h].ap, func=AF.Copy), reads=[Sst[h]], writes=[Sb[h][0]])
                    op(ACT, lambda e, bo=bo: e.activation(out=oT.ap, in_=bo.ap, func=AF.Copy), reads=[bo], writes=[oT])
                    op(ACT, lambda e, bo=bo: e.activation(out=osq.ap, in_=bo.ap, func=AF.Square), reads=[bo], writes=[osq])
                    bs = kb.bank()
                    mm(bs, bs.ap, onesb.ap, osq.ap, True, True, [onesb, osq], inc=True)
                    op(ACT, lambda e, bs=bs: e.activation(out=bs.ap, in_=bs.ap, func=AF.Ln, scale=1.0 / 128, bias=EPS), reads=[bs], writes=[bs])
                    op(ACT, lambda e, bs=bs: e.activation(out=bs.ap, in_=bs.ap, func=AF.Exp, scale=-0.5), reads=[bs], writes=[bs])
                    op(DVE, lambda e, bs=bs: e.scalar_tensor_tensor(out=t1.ap, in0=oT.ap, scalar=recg.ap[:, 0:1], in1=bs.ap, op0=ALU.mult, op1=ALU.mult),
                       reads=[oT, recg, bs], writes=[t1])
                    bg_ = kb.bank()
                    for k in range(8):
                        mm(bg_, bg_.ap, vg[:, k, h * 128:(h + 1) * 128], uT[k].ap, k == 0, k == 7, [slot_rg, uT[k]], inc=(k == 7))
                    sr = sgr[h % 2]
                    op(ACT, lambda e, bg_=bg_, sr=sr: e.activation(out=sr.ap, in_=bg_.ap, func=AF.Sigmoid), reads=[bg_], writes=[sr])
                    op(DVE, lambda e, sr=sr, h=h: e.tensor_tensor(out=recT[h].ap, in0=t1.ap, in1=sr.ap, op=ALU.mult), reads=[t1, sr], writes=[recT[h]])

                for m in range(8):
                    slot = load_piece(f"ph2{m}")
                    gab = slot.ap[:, 0:2048].rearrange("p (k q c) -> p k q c", k=8, q=2)
                    wa_ = slot.ap[:, 2048:2560].rearrange("p (c m) -> p c m", c=4)
                    wr_ = slot.ap[:, 2560:3072].rearrange("p (c m) -> p c m", c=4)
                    bga = kb.bank()
                    for k in range(8):
                        mm(bga, bga.ap, gab[:, k, 0, :], uT[k].ap, k == 0, k == 7, [slot, uT[k]], inc=(k == 7))
                    op(ACT, lambda e, bga=bga: e.activation(out=sga.ap, in_=bga.ap, func=AF.Sigmoid), reads=[bga], writes=[sga])
                    bgb = kb.bank()
                    for k in range(8):
                        mm(bgb, bgb.ap, gab[:, k, 1, :], uT[k].ap, k == 0, k == 7, [slot, uT[k]], inc=(k == 7))
                    op(ACT, lambda e, bgb=bgb: e.activation(out=sgb.ap, in_=bgb.ap, func=AF.Sigmoid), reads=[bgb], writes=[sgb])
                    bya = kb.bank()
                    for g in range(4):
                        mm(bya, bya.ap, wa_[:, g, :], attT.ap[:, g, :], g == 0, g == 3, [slot, attT], inc=(g == 3))
                    op(DVE, lambda e, bya=bya: e.tensor_tensor(out=m1.ap, in0=bya.ap, in1=sga.ap, op=ALU.mult), reads=[bya, sga], writes=[m1])
                    byb = kb.bank()
                    for h in range(4):
                        mm(byb, byb.ap, wr_[:, h, :], recT[h].ap, h == 0, h == 3, [slot, recT[h]], inc=(h == 3))
                    op(DVE, lambda e, byb=byb: e.tensor_tensor(out=m2.ap, in0=byb.ap, in1=sgb.ap, op=ALU.mult), reads=[byb, sgb], writes=[m2])
                    op(DVE, lambda e, m=m: e.tensor_tensor(out=mT[m].ap, in0=m1.ap, in1=m2.ap, op=ALU.add), reads=[m1, m2], writes=[mT[m]])
                for i in range(2):
                    slot = load_piece(f"wo{i}")
                    v3 = slot.ap.rearrange("p (k c) -> p k c", k=8)
                    for mm_ in range(4):
                        m = i * 4 + mm_
                        bk = kb.bank()
                        for k in range(8):
                            mm(bk, bk.ap, v3[:, k, mm_ * 128:(mm_ + 1) * 128], mT[k].ap, k == 0, k == 7, [slot, mT[k]], inc=(k == 7))
                        op(DVE, lambda e, bk=bk, m=m: e.tensor_tensor(out=hT[m].ap, in0=bk.ap, in1=hT[m].ap, op=ALU.add), reads=[bk, hT[m]], writes=[hT[m]])

            if stop_after is None or stop_after >= 3:
                ffn("f2", 2)

            if stop_after is None or stop_after >= 4:
                norm(3, uT)
                slot_pp = load_piece("pp")
                vp = slot_pp.ap[:, 0:2048].rearrange("p (k c) -> p k c", k=2)
                for i in range(2):
                    slot = load_piece(f"pg{i}")
                    v3 = slot.ap.rearrange("p (k c) -> p k c", k=8)
                    for mm_ in range(4):
                        m = i * 4 + mm_
                        bk = kb.bank()
                        for k in range(8):
                            mm(bk, bk.ap, v3[:, k, mm_ * 128:(mm_ + 1) * 128], uT[k].ap, k == 0, k == 7, [slot, uT[k]], inc=(k == 7))
                        op(ACT, lambda e, bk=bk: e.activation(out=sga.ap, in_=bk.ap, func=AF.Sigmoid), reads=[bk], writes=[sga])
                        bp = kb.bank()
                        for k in range(2):
                            mm(bp, bp.ap, vp[:, k, m * 128:(m + 1) * 128], pT[k].ap, k == 0, k == 1, [slot_pp, pT[k]], inc=(k == 1))
                        op(DVE, lambda e, bp=bp: e.tensor_tensor(out=m1.ap, in0=bp.ap, in1=sga.ap, op=ALU.mult), reads=[bp, sga], writes=[m1])
                        op(DVE, lambda e, m=m: e.tensor_tensor(out=hT[m].ap, in0=m1.ap, in1=hT[m].ap, op=ALU.add), reads=[m1, hT[m]], writes=[hT[m]])

            if stop_after is None:
                norm(4, hT)

            for n in range(4):
                o = ob[n % 2]
                for half in range(2):
                    bk = kb.bank()
                    for cc in range(4):
                        c = half * 4 + cc
                        op(PE, lambda e, bk=bk, cc=cc, c=c, n=n: e.transpose(bk.ap[:, cc * 128:(cc + 1) * 128], hT[c].ap[:, n * 128:(n + 1) * 128], ident.ap),
                           reads=[hT[c], ident], writes=[bk], inc=(cc == 3))
                    if half == 0:
                        op(DVE, lambda e, bk=bk, o=o: e.tensor_copy(out=o.ap[:, 0:512], in_=bk.ap), reads=[bk], writes=[o])
                    else:
                        op(ACT, lambda e, bk=bk, o=o: e.activation(out=o.ap[:, 512:1024], in_=bk.ap, func=AF.Copy), reads=[bk], writes=[o])
                dma(GPQ, y[tok0 + n * 128: tok0 + (n + 1) * 128, :], o.ap, o, reads=[o])

        for o in ob:
            GPQ.wait((o.dsem["gp"], o.dcount["gp"]))
    return kb


def _host_consts(inputs):
    f = np.float32
    g5 = [inputs["norm_ffn1"][0], inputs["norm_mix"][0], inputs["norm_ffn2"][0], inputs["norm_ple"][0], inputs["norm_final"]]
    gains = np.stack([np.asarray(g, f).reshape(8, 128).T for g in g5], axis=1).reshape(128, 40)
    recg = np.asarray(inputs["rec_norm"][0], f).reshape(128, 1)
    lbp = np.asarray(inputs["lb_param"], f).reshape(2, 4, 128).transpose(2, 0, 1).reshape(128, 8)
    sk = np.asarray(inputs["attn_sinks"][0], f)
    sinks = np.stack([sk[0:4]] * 64 + [sk[4:8]] * 64, axis=0)
    qi = np.arange(128)[:, None] + 128
    kj = np.arange(256)[None, :]
    dist = qi - kj
    nn = np.maximum(dist, 0)
    large = 16 + (np.log(np.maximum(nn, 1) / 16) / np.log(128 / 16) * 16).astype(np.int32)
    large = np.minimum(large, 31)
    bucket = np.where(nn < 16, nn, large).astype(np.int32)
    valid = (dist >= 0) & (dist < 128)
    rb = np.asarray(inputs["rel_bias"], f)
    bias = rb[bucket]
    bias = np.where(valid[:, :, None], bias, f(NEG))
    b5 = bias.reshape(128, 2, 128, 2, 4)
    biasm = np.ascontiguousarray(b5.transpose(2, 3, 1, 4, 0)).reshape(128, 2048).astype(f)
    ident = np.eye(128, dtype=f)
    s_ = np.arange(128)[:, None]
    t_ = np.arange(128)[None, :]
    cmask = ((s_ // 64 == t_ // 64) & (s_ <= t_)).astype(f)
    return dict(gains=np.ascontiguousarray(gains), recg=recg, lbp=np.ascontiguousarray(lbp), sinks=np.ascontiguousarray(sinks),
                biasm=biasm, ident=ident, cmask=cmask)


def _weights(inputs):
    f = np.float32
    return dict(
        w1i=np.asarray(inputs["w_ffn1_in"][0], f), w1o=np.asarray(inputs["w_ffn1_out"][0], f),
        w2i=np.asarray(inputs["w_ffn2_in"][0], f), w2o=np.asarray(inputs["w_ffn2_out"][0], f),
        win=np.asarray(inputs["w_in"][0], f), watt=np.asarray(inputs["w_att_proj"][0], f),
        wrec=np.asarray(inputs["w_rec_proj"][0], f), wout=np.asarray(inputs["w_out"][0], f),
        wpg=np.asarray(inputs["w_ple_gate"][0], f), wpp=np.asarray(inputs["w_ple_proj"][0], f))


def kernel(**inputs):
    x = np.asarray(inputs["x"], np.float32)
    p = np.asarray(inputs["p"], np.float32)[0]
    B, S, _ = x.shape
    ncores = 8
    nseq = B // ncores
    kb = build(nseq, S)
    shared = {}
    shared.update(_host_consts(inputs))
    shared.update(_weights(inputs))
    in_maps = []
    for i in range(ncores):
        m = dict(shared)
        m["x"] = np.ascontiguousarray(x[i * nseq:(i + 1) * nseq].reshape(nseq * S, D))
        m["p"] = np.ascontiguousarray(p[i * nseq:(i + 1) * nseq].reshape(nseq * S, 256))
        in_maps.append(m)
    res = run_bass_kernel_spmd(kb.nc, in_maps, core_ids=list(range(ncores)))
    out = np.concatenate([np.asarray(r["y"]).reshape(nseq, S, D) for r in res.results], axis=0)
    return out.astype(np.float32)
```
